# Optimizing a Trainium2 kernel written in Bass

```python
import math
import jax, jax.numpy as jnp
from jax import lax
import numpy as np

D_MODEL = 1024
BATCH = 2
SEQ = 16384
DEPTH = 2

N_A_LAYERS = DEPTH // 2
N_B_LAYERS = DEPTH - N_A_LAYERS
MEM_TOKENS = 256
D_FF = 2816
HEAD_DIM = 64
MEM_HEADS = 4
MEM_WIDTH = MEM_HEADS * HEAD_DIM
MLSTM_HEADS = 4
MLSTM_HEAD_DIM = 192
MLSTM_WIDTH = MLSTM_HEADS * MLSTM_HEAD_DIM
MLSTM_CHUNK = 128
CONV_WIDTH = 4
DILATED_GROUPS = ((128, 1), (512, 4), (2048, 16))
HEADS_PER_GROUP = 4
N_DIL_HEADS = HEADS_PER_GROUP * len(DILATED_GROUPS)
DIL_WIDTH = N_DIL_HEADS * HEAD_DIM
A_IN_WIDTH = 4 * MLSTM_WIDTH + 2 * MLSTM_HEADS + MEM_WIDTH
B_IN_WIDTH = DIL_WIDTH + MEM_WIDTH
NUM_BUCKETS = 32
MAX_DISTANCE = 2048
RMS_EPS = 1e-6
NEG_INF = -1e30
ATTN_SCALE = HEAD_DIM ** -0.5

kernel_name = 'yoco_mlstm_dilated_macaron_hybrid'


def rms_norm(x, gain):
    xf = x.astype(jnp.float32)
    y = xf * lax.rsqrt(jnp.mean(xf * xf, axis=-1, keepdims=True) + RMS_EPS)
    return (y * gain.astype(jnp.float32)).astype(x.dtype)


def swiglu(h, w_gate, w_up, w_down):
    return (jax.nn.silu(h @ w_gate) * (h @ w_up)) @ w_down


def causal_conv(x, w):
    C = x.shape[-1]
    return lax.conv_general_dilated(x, w[:, None, :].astype(x.dtype), window_strides=(1,),
                                    padding=[(w.shape[0] - 1, 0)],
                                    dimension_numbers=('NWC', 'WIO', 'NWC'),
                                    feature_group_count=C)


def t5_causal_bucket(dist):
    max_exact = NUM_BUCKETS // 2
    d_f = jnp.maximum(dist, 1).astype(jnp.float32)
    large = max_exact + (jnp.log(d_f / max_exact) / math.log(MAX_DISTANCE / max_exact)
                         * (NUM_BUCKETS - max_exact)).astype(jnp.int32)
    large = jnp.minimum(large, NUM_BUCKETS - 1)
    return jnp.where(dist < max_exact, dist, large)


def mlstm_chunkwise(q, k, v, i_pre, f_pre):
    B, S, H, DK = q.shape
    DV = v.shape[-1]
    L = MLSTM_CHUNK
    NC = S // L
    f32 = jnp.float32

    def chunks(t):
        t = t.astype(f32).reshape(B, NC, L, H, *t.shape[3:])
        return jnp.moveaxis(t, 3, 2)

    qc = chunks(q)
    kc = chunks(k) * (DK ** -0.5)
    vc = chunks(v)
    ic = chunks(i_pre)
    logf = jax.nn.log_sigmoid(chunks(f_pre))
    b = jnp.cumsum(logf, axis=-1)
    b_last = b[..., -1]
    a = b_last[..., None] - b + ic

    def step(carry, xs):
        C, n, m = carry
        k_j, v_j, a_j, bl_j = xs
        m_new = jnp.maximum(bl_j + m, jnp.max(a_j, axis=-1))
        w = jnp.exp(a_j - m_new[..., None])
        decay = jnp.exp(bl_j + m - m_new)
        C_new = decay[..., None, None] * C + jnp.einsum('bhl,bhlk,bhlv->bhkv', w, k_j, v_j)
        n_new = decay[..., None] * n + jnp.einsum('bhl,bhlk->bhk', w, k_j)
        return (C_new, n_new, m_new), (C, n, m)

    init = (jnp.zeros((B, H, DK, DV), f32), jnp.zeros((B, H, DK), f32), jnp.zeros((B, H), f32))
    xs = (jnp.moveaxis(kc, 1, 0), jnp.moveaxis(vc, 1, 0), jnp.moveaxis(a, 1, 0), jnp.moveaxis(b_last, 1, 0))
    _, (C0, n0, m0) = lax.scan(step, init, xs)
    C0 = jnp.moveaxis(C0, 0, 1)
    n0 = jnp.moveaxis(n0, 0, 1)
    m0 = jnp.moveaxis(m0, 0, 1)

    causal = jnp.tril(jnp.ones((L, L), dtype=bool))
    Dlog = jnp.where(causal, b[..., :, None] - b[..., None, :] + ic[..., None, :], NEG_INF)
    g = b + m0[..., None]
    m_t = jnp.maximum(g, jnp.max(Dlog, axis=-1))
    W = jnp.exp(Dlog - m_t[..., None]) * jnp.einsum('bchtk,bchsk->bchts', qc, kc)
    inter = jnp.exp(g - m_t)
    num = jnp.einsum('bchts,bchsv->bchtv', W, vc) + inter[..., None] * jnp.einsum('bchtk,bchkv->bchtv', qc, C0)
    den = jnp.sum(W, axis=-1) + inter * jnp.einsum('bchtk,bchk->bcht', qc, n0)
    h = num / jnp.maximum(jnp.abs(den), jnp.exp(-m_t))[..., None]
    return jnp.moveaxis(h, 2, 3).reshape(B, S, H, DV)


def dilated_group_attention(q, k, v, bias_table, window, dilation):
    B, S, H, E = q.shape
    blk = window // dilation
    span = window
    Lp = -(-S // span) * span
    nb = Lp // span
    pad = ((0, 0), (0, Lp - S), (0, 0), (0, 0))

    def to_blocks(t):
        t = jnp.pad(t.astype(jnp.float32), pad).reshape(B, Lp // dilation, dilation, H, E)
        return jnp.transpose(t, (0, 2, 1, 3, 4)).reshape(B, dilation, nb, blk, H, E)

    qb = to_blocks(q)
    kb = to_blocks(k)
    vb = to_blocks(v)
    prev = lambda t: jnp.pad(t, ((0, 0), (0, 0), (1, 0), (0, 0), (0, 0), (0, 0)))[:, :, :-1]
    kk = jnp.concatenate([prev(kb), kb], axis=3)
    vv = jnp.concatenate([prev(vb), vb], axis=3)

    qi = jnp.arange(blk)[:, None]
    kj = jnp.arange(2 * blk)[None, :]
    m_off = qi + blk - kj
    band = (m_off >= 0) & (m_off <= blk)
    bucket = t5_causal_bucket(jnp.clip(m_off, 0, blk) * dilation)
    bias = jnp.transpose(bias_table.astype(jnp.float32)[bucket], (2, 0, 1))
    n_idx = jnp.arange(nb)[:, None, None]
    allowed = band[None] & ~((n_idx == 0) & (kj[None] < blk))

    s = jnp.einsum('brnqhe,brnkhe->brnhqk', qb, kk) * ATTN_SCALE + bias
    s = jnp.where(allowed[:, None], s, NEG_INF)
    mx = jnp.max(s, axis=-1, keepdims=True)
    p = jnp.exp(s - mx)
    den = jnp.sum(p, axis=-1)
    out = jnp.einsum('brnhqk,brnkhe->brnqhe', p, vv) / jnp.moveaxis(den, 3, 4)[..., None]
    lse = mx[..., 0] + jnp.log(den)

    out = jnp.transpose(out.reshape(B, dilation, Lp // dilation, H, E), (0, 2, 1, 3, 4)).reshape(B, Lp, H, E)[:, :S]
    lse = jnp.moveaxis(lse, 3, 4).reshape(B, dilation, Lp // dilation, H)
    lse = jnp.transpose(lse, (0, 2, 1, 3)).reshape(B, Lp, H)[:, :S]
    return out, lse


def memory_cross_attention(q_mem, mem_n, w_mem_kv, q_gain, k_gain):
    B, S, _ = q_mem.shape
    M = mem_n.shape[1]
    q = rms_norm(q_mem.reshape(B, S, MEM_HEADS, HEAD_DIM), q_gain)
    kv = mem_n @ w_mem_kv
    k = rms_norm(kv[..., :MEM_WIDTH].reshape(B, M, MEM_HEADS, HEAD_DIM), k_gain)
    v = kv[..., MEM_WIDTH:].reshape(B, M, MEM_HEADS, HEAD_DIM)
    s = jnp.einsum('bshe,bmhe->bhsm', q.astype(jnp.float32), k.astype(jnp.float32)) * ATTN_SCALE
    p = jax.nn.softmax(s, axis=-1)
    out = jnp.einsum('bhsm,bmhe->bshe', p, v.astype(jnp.float32))
    return out.reshape(B, S, MEM_WIDTH).astype(q_mem.dtype)


def mlstm_mixer(h, mem_n, w_in, conv_w, gate_bias, h_gain, w_out, w_mem_kv, mq_gain, mk_gain):
    B, S, _ = h.shape
    W = MLSTM_WIDTH
    p = h @ w_in
    qk = jax.nn.silu(causal_conv(p[..., :2 * W], conv_w))
    v = p[..., 2 * W:3 * W]
    o = p[..., 3 * W:4 * W]
    gates = p[..., 4 * W:4 * W + 2 * MLSTM_HEADS].astype(jnp.float32) + gate_bias.astype(jnp.float32)
    q_mem = p[..., 4 * W + 2 * MLSTM_HEADS:]
    heads = lambda t: t.reshape(B, S, MLSTM_HEADS, MLSTM_HEAD_DIM)
    cell = mlstm_chunkwise(heads(qk[..., :W]), heads(qk[..., W:]), heads(v),
                           gates[..., :MLSTM_HEADS], gates[..., MLSTM_HEADS:])
    cell = rms_norm(cell, h_gain).reshape(B, S, W)
    h_a = (jax.nn.sigmoid(o.astype(jnp.float32)) * cell).astype(h.dtype)
    h_m = memory_cross_attention(q_mem, mem_n, w_mem_kv, mq_gain, mk_gain)
    return jnp.concatenate([h_a, h_m], axis=-1) @ w_out


def dilated_mixer(h, mem_n, k_sh, v_sh, w_q, q_gain, w_out, rel_bias, w_mem_kv, mq_gain, mk_gain):
    B, S, _ = h.shape
    p = h @ w_q
    q_d = rms_norm(p[..., :DIL_WIDTH].reshape(B, S, len(DILATED_GROUPS), HEADS_PER_GROUP, HEAD_DIM), q_gain)
    outs, lses = [], []
    for g, (window, dilation) in enumerate(DILATED_GROUPS):
        table = rel_bias[:, g * HEADS_PER_GROUP:(g + 1) * HEADS_PER_GROUP]
        o_g, l_g = dilated_group_attention(q_d[:, :, g], k_sh[:, :, g], v_sh[:, :, g], table, window, dilation)
        outs.append(o_g)
        lses.append(l_g)
    outs = jnp.stack(outs, axis=2)
    alpha = jax.nn.softmax(jnp.stack(lses, axis=2), axis=2)
    h_d = (outs * alpha[..., None]).reshape(B, S, DIL_WIDTH).astype(h.dtype)
    h_m = memory_cross_attention(p[..., DIL_WIDTH:], mem_n, w_mem_kv, mq_gain, mk_gain)
    return jnp.concatenate([h_d, h_m], axis=-1) @ w_out


def setup_inputs(seed: int = 0) -> dict:
    key = jax.random.key(seed)
    ks = iter(jax.random.split(key, 40))
    nrm = lambda shape, scale: jax.random.normal(next(ks), shape, jnp.float32) * scale
    gain = lambda shape: 1.0 + nrm(shape, 0.02)
    D, F = D_MODEL, D_FF
    return {
        'x': nrm((BATCH, SEQ, D), 1.0),
        'mem': nrm((BATCH, MEM_TOKENS, D), 1.0),
        'ffn1_norm': gain((DEPTH, D)),
        'ffn1_w_gate': nrm((DEPTH, D, F), D ** -0.5),
        'ffn1_w_up': nrm((DEPTH, D, F), D ** -0.5),
        'ffn1_w_down': nrm((DEPTH, F, D), F ** -0.5),
        'ffn2_norm': gain((DEPTH, D)),
        'ffn2_w_gate': nrm((DEPTH, D, F), D ** -0.5),
        'ffn2_w_up': nrm((DEPTH, D, F), D ** -0.5),
        'ffn2_w_down': nrm((DEPTH, F, D), F ** -0.5),
        'mix_norm': gain((DEPTH, D)),
        'mem_norm': gain((DEPTH, D)),
        'w_mem_kv': nrm((DEPTH, D, 2 * MEM_WIDTH), D ** -0.5),
        'mem_q_norm': gain((DEPTH, HEAD_DIM)),
        'mem_k_norm': gain((DEPTH, HEAD_DIM)),
        'a_w_in': nrm((N_A_LAYERS, D, A_IN_WIDTH), D ** -0.5),
        'a_conv': nrm((N_A_LAYERS, CONV_WIDTH, 2 * MLSTM_WIDTH), CONV_WIDTH ** -0.5),
        'a_gate_bias': jnp.concatenate([nrm((N_A_LAYERS, MLSTM_HEADS), 0.1),
                                        3.0 + nrm((N_A_LAYERS, MLSTM_HEADS), 0.5)], axis=-1),
        'a_h_norm': gain((N_A_LAYERS, MLSTM_HEADS, MLSTM_HEAD_DIM)),
        'a_w_out': nrm((N_A_LAYERS, MLSTM_WIDTH + MEM_WIDTH, D), (MLSTM_WIDTH + MEM_WIDTH) ** -0.5),
        'b_w_q': nrm((N_B_LAYERS, D, B_IN_WIDTH), D ** -0.5),
        'b_q_norm': gain((N_B_LAYERS, HEAD_DIM)),
        'b_w_out': nrm((N_B_LAYERS, DIL_WIDTH + MEM_WIDTH, D), (DIL_WIDTH + MEM_WIDTH) ** -0.5),
        'kv_norm': gain((D,)),
        'w_kv': nrm((D, 2 * DIL_WIDTH), D ** -0.5),
        'kv_k_norm': gain((HEAD_DIM,)),
        'rel_bias': nrm((NUM_BUCKETS, N_DIL_HEADS), 0.3),
    }


def reference(x, mem, ffn1_norm, ffn1_w_gate, ffn1_w_up, ffn1_w_down,
              ffn2_norm, ffn2_w_gate, ffn2_w_up, ffn2_w_down,
              mix_norm, mem_norm, w_mem_kv, mem_q_norm, mem_k_norm,
              a_w_in, a_conv, a_gate_bias, a_h_norm, a_w_out,
              b_w_q, b_q_norm, b_w_out, kv_norm, w_kv, kv_k_norm, rel_bias):
    B, S, _ = x.shape
    n_groups = len(DILATED_GROUPS)
    k_sh = None
    v_sh = None
    for layer in range(DEPTH):
        x = x + 0.5 * swiglu(rms_norm(x, ffn1_norm[layer]), ffn1_w_gate[layer], ffn1_w_up[layer], ffn1_w_down[layer])
        h = rms_norm(x, mix_norm[layer])
        mem_n = rms_norm(mem, mem_norm[layer])
        if layer < N_A_LAYERS:
            ia = layer
            mix = mlstm_mixer(h, mem_n, a_w_in[ia], a_conv[ia], a_gate_bias[ia], a_h_norm[ia], a_w_out[ia],
                              w_mem_kv[layer], mem_q_norm[layer], mem_k_norm[layer])
        else:
            ib = layer - N_A_LAYERS
            mix = dilated_mixer(h, mem_n, k_sh, v_sh, b_w_q[ib], b_q_norm[ib], b_w_out[ib], rel_bias,
                                w_mem_kv[layer], mem_q_norm[layer], mem_k_norm[layer])
        x = x + mix
        x = x + 0.5 * swiglu(rms_norm(x, ffn2_norm[layer]), ffn2_w_gate[layer], ffn2_w_up[layer], ffn2_w_down[layer])
        if layer == N_A_LAYERS - 1:
            kv = rms_norm(x, kv_norm) @ w_kv
            k_sh = rms_norm(kv[..., :DIL_WIDTH].reshape(B, S, N_DIL_HEADS, HEAD_DIM), kv_k_norm)
            k_sh = k_sh.reshape(B, S, n_groups, HEADS_PER_GROUP, HEAD_DIM)
            v_sh = kv[..., DIL_WIDTH:].reshape(B, S, n_groups, HEADS_PER_GROUP, HEAD_DIM)
    return x
```

```python
from contextlib import ExitStack

import numpy as np
import concourse.bass as bass
import concourse.mybir as mybir
from concourse.alu_op_type import AluOpType as ALU
from concourse.bass_utils import run_bass_kernel_spmd

F32 = mybir.dt.float32
BF16 = mybir.dt.bfloat16
I32 = mybir.dt.int32
AF = mybir.ActivationFunctionType
AX = mybir.AxisListType

P = 128
D = 1024
FF = 2816
NFC = FF // P
NCORE = 8
TOK = 4096
NT = TOK // P
EPS = 1e-6

ENGS = ["pe", "act", "dve", "pool", "sp"]
ENG_ATTR = {"pe": "tensor", "act": "scalar", "dve": "vector", "pool": "gpsimd", "sp": "sync"}
N_DMA_SEMS = 8


class Res:
    __slots__ = ("name", "w", "r", "psum")

    def __init__(self, name):
        self.name = name
        self.psum = False
        self.w = None
        self.r = {}


class Sched:
    def __init__(self):
        self.prog = {e: [] for e in ENGS}
        self.cnt = {e: 0 for e in ENGS}
        self.seen = {e: {} for e in ENGS}
        self.dma_cnt = [0] * N_DMA_SEMS
        self.dma_rr = 0

    def _wait(self, eng, evs):
        for k, v in evs:
            if k == eng and eng == "pe":
                continue
            if self.seen[eng].get(k, 0) >= v:
                continue
            self.seen[eng][k] = v
            self.prog[eng].append(("wait", k, v))

    @staticmethod
    def _deps(reads, writes):
        evs = {}

        def add(ev):
            if ev is None:
                return
            k, v = ev
            if evs.get(k, 0) < v:
                evs[k] = v
        for r in reads:
            add(r.w)
            if r.psum:
                for k, v in r.r.items():
                    add((k, v))
        for w in writes:
            add(w.w)
            for k, v in w.r.items():
                add((k, v))
        return list(evs.items())

    def op(self, eng, fn, reads=(), writes=(), signal=True):
        assert signal or eng == "pe"
        self._wait(eng, self._deps(reads, writes))
        if signal:
            self.cnt[eng] += 1
            ev = (eng, self.cnt[eng])
        else:
            ev = (eng, self.cnt[eng] + 1)
        self.prog[eng].append(("op", fn, signal))
        for r in reads:
            if r.r.get(eng, 0) < ev[1]:
                r.r[eng] = ev[1]
        for w in writes:
            w.w = ev
            w.r = {}

    def dma(self, queue, fn, reads=(), writes=()):
        k = self.dma_rr
        self.dma_rr = (k + 1) % N_DMA_SEMS
        key = ("dma", k)
        evs = self._deps(reads, writes)
        if self.dma_cnt[k] > 0:
            evs.append((key, 16 * self.dma_cnt[k]))
        self._wait(queue, evs)
        self.dma_cnt[k] += 1
        ev = (key, 16 * self.dma_cnt[k])
        self.prog[queue].append(("dma", fn, key))
        for r in reads:
            if r.r.get(key, 0) < ev[1]:
                r.r[key] = ev[1]
        for w in writes:
            w.w = ev
            w.r = {}

    def barrier(self):
        evs = [(e, self.cnt[e]) for e in ENGS if self.cnt[e] > 0]
        evs += [(("dma", k), 16 * c) for k, c in enumerate(self.dma_cnt) if c > 0]
        for e in ENGS:
            self._wait(e, [ev for ev in evs if ev[0] != e])

    def finish(self, eng="sp"):
        evs = [(e, self.cnt[e]) for e in ENGS if self.cnt[e] > 0 and e != eng]
        evs += [(("dma", k), 16 * c) for k, c in enumerate(self.dma_cnt) if c > 0]
        self._wait(eng, evs)

    def emit(self, nc):
        with ExitStack() as st:
            sem = {}
            for e in ENGS:
                sem[e] = st.enter_context(nc.semaphore("s_" + e))
            for k in range(N_DMA_SEMS):
                sem[("dma", k)] = st.enter_context(nc.semaphore("d%d" % k))
            with nc.Block() as block:
                for e in ENGS:
                    prog = self.prog[e]
                    if not prog:
                        continue

                    def body(engine, prog=prog, e=e):
                        for it in prog:
                            if it[0] == "wait":
                                engine.wait_ge(sem[it[1]], it[2])
                            elif it[0] == "op":
                                ins = it[1](engine)
                                if it[2]:
                                    ins.then_inc(sem[e], 1)
                            else:
                                it[1](engine).then_inc(sem[it[2]], 16)
                    getattr(block, ENG_ATTR[e])(body)


class Ring:
    def __init__(self, items):
        self.items = items
        self.i = 0

    def next(self):
        it = self.items[self.i % len(self.items)]
        self.i += 1
        return it


class Tile:
    __slots__ = ("ap", "res")

    def __init__(self, ap, name):
        self.ap = ap
        self.res = Res(name)


class Prog:
    NF32 = 10400
    NBF = 14400
    WSLOT = 24576

    def __init__(self):
        self.nc = bass.Bass("TRN2", target_bir_lowering=False)
        self.s = Sched()
        self.st = ExitStack()
        nc = self.nc
        e = self.st.enter_context
        self.ident = Tile(e(nc.sbuf_tensor("ident", [P, P], BF16))[:], "ident")
        self.wslot = [Tile(e(nc.sbuf_tensor("wslot%d" % i, [P, self.WSLOT], BF16))[:], "wslot%d" % i)
                      for i in range(2)]
        self.af = e(nc.sbuf_tensor("arena_f", [P, self.NF32], F32))
        self.ab = e(nc.sbuf_tensor("arena_b", [P, self.NBF], BF16))
        self.cf = e(nc.sbuf_tensor("const_f", [P, 64], F32))
        self.ps_bf = Tile(e(nc.psum_tensor("ps_bf", [P, 1024], BF16))[:], "ps_bf")
        self.ps = [Tile(e(nc.psum_tensor("ps%d" % i, [P, 512], F32))[:], "ps%d" % i) for i in range(7)]
        self.ps_bf.res.psum = True
        for t in self.ps:
            t.res.psum = True
        self.dram = {}
        self.dres = {}
        self.off_f = 0
        self.off_b = 0

    def phase(self):
        self.s.barrier()
        self.off_f = 0
        self.off_b = 0

    def tf(self, n, name, shape=None):
        assert self.off_f + n <= self.NF32, (name, self.off_f, n)
        ap = self.af[:, self.off_f:self.off_f + n]
        self.off_f += n
        if shape:
            ap = ap.rearrange(shape[0], **shape[1])
        return Tile(ap, name)

    def tb(self, n, name, shape=None):
        assert self.off_b + n <= self.NBF, (name, self.off_b, n)
        ap = self.ab[:, self.off_b:self.off_b + n]
        self.off_b += n
        if shape:
            ap = ap.rearrange(shape[0], **shape[1])
        return Tile(ap, name)

    def dt(self, name, shape, dtype, kind="Internal"):
        t = self.nc.dram_tensor(name, list(shape), dtype, kind=kind)
        self.dram[name] = t
        return t.ap()

    def dr(self, name, i=0):
        k = (name, i)
        if k not in self.dres:
            self.dres[k] = Res("%s[%s]" % k)
        return self.dres[k]

    def finish(self):
        self.s.finish("sp")
        self.s.emit(self.nc)
        self.st.close()
        return self.nc


def setup_consts(pg, ident_dram):
    s = pg.s
    idt = pg.ident
    s.dma("sp", lambda e: e.dma_start(out=idt.ap, in_=ident_dram), [], [idt.res])
    cres = Res("const_f")
    pg.cres = cres
    cf = pg.cf
    s.op("pool", lambda e: e.memset(cf[:, 0:1], -0.5), [], [cres])
    pg.neghalf = cf[:, 0:1]


def load_ffn_weights(pg, slot, wg, wu, wd, c0, c1, queue="pool"):
    s = pg.s
    ncp = c1 - c0
    w = pg.wslot[slot]
    n1 = 8 * ncp * P
    wg_s = w.ap[:, 0:n1].rearrange("p (k c) -> p k c", k=8)
    wu_s = w.ap[:, n1:2 * n1].rearrange("p (k c) -> p k c", k=8)
    wd_s = w.ap[:, 2 * n1:2 * n1 + ncp * D].rearrange("p (f d) -> p f d", d=D)
    wg_v = wg.rearrange("(k p) c -> p k c", p=P)
    wu_v = wu.rearrange("(k p) c -> p k c", p=P)
    wd_v = wd.rearrange("(f p) d -> p f d", p=P)
    rg = [Res("wg%d" % k) for k in range(8)]
    ru = [Res("wu%d" % k) for k in range(8)]
    rd = [Res("wd%d" % f) for f in range(ncp)]
    for r in rg + ru + rd:
        r.r = dict(w.res.r)
        r.w = w.res.w
    for k in range(8):
        s.dma(queue, lambda e, k=k: e.dma_start(out=wg_s[:, k, :], in_=wg_v[:, k, c0 * P:c1 * P]), [], [rg[k]])
        s.dma(queue, lambda e, k=k: e.dma_start(out=wu_s[:, k, :], in_=wu_v[:, k, c0 * P:c1 * P]), [], [ru[k]])
    for f in range(ncp):
        s.dma(queue, lambda e, f=f: e.dma_start(out=wd_s[:, f, :], in_=wd_v[:, c0 + f, :]), [], [rd[f]])
    return (wg_s, wu_s, wd_s, rg, ru, rd)


def ffn_pass(pg, xname, x_src, accname, acc_src, dstname, dst, wviews, ncp, gain_dram, ntiles):
    s = pg.s
    wg_s, wu_s, wd_s, rg, ru, rd = wviews
    gain = pg.tf(D, "gain")
    s.dma("sp", lambda e: e.dma_start(out=gain.ap, in_=gain_dram.partition_broadcast(P)), [], [gain.res])
    xt = Ring([pg.tf(D, "xt%d" % i) for i in range(3)])
    at = Ring([pg.tf(D, "at%d" % i) for i in range(2)])
    ot = Ring([pg.tf(D, "ot%d" % i) for i in range(2)])
    sg = Ring([pg.tf(512, "sg%d" % i) for i in range(2)])
    st_ = Ring([pg.tf(4, "st%d" % i) for i in range(4)])
    yb = Ring([pg.tb(D, "yb%d" % i) for i in range(2)])
    hT = Ring([pg.tb(8 * 512, "hT%d" % i, ("p (k t) -> p k t", dict(k=8))) for i in range(2)])
    aT = Ring([pg.tb(ncp * 512, "aT%d" % i, ("p (f t) -> p f t", dict(f=ncp))) for i in range(1)])
    psb = pg.ps_bf
    ps_g = Ring([pg.ps[0], pg.ps[1]])
    ps_u = Ring([pg.ps[2], pg.ps[3]])
    ps_o = Ring([pg.ps[4], pg.ps[5], pg.ps[6]])
    ident = pg.ident
    neghalf = pg.neghalf

    sts = []
    t0 = 0
    while t0 < ntiles:
        nt = min(4, ntiles - t0)
        sts.append((t0, nt))
        t0 += nt

    ystate = {}

    def front_a(h, t0, j):
        ti = t0 + j
        x = xt.next()
        s.dma("sp", lambda e: e.dma_start(out=x.ap, in_=x_src[ti * P:(ti + 1) * P, :]),
              [pg.dr(xname, ti)], [x.res])
        y = yb.next()
        stt = st_.next()
        s.op("act", lambda e: e.activation(out=y.ap, in_=x.ap, func=AF.Square, accum_out=stt.ap[:, 0:1]),
             [x.res], [y.res, stt.res])
        s.op("pool", lambda e: e.tensor_scalar(out=stt.ap[:, 1:2], in0=stt.ap[:, 0:1],
                                               scalar1=1.0 / D, scalar2=EPS, op0=ALU.mult, op1=ALU.add),
             [stt.res], [stt.res])
        s.op("pool", lambda e: e.tensor_tensor(out=stt.ap[:, 2:3], in0=stt.ap[:, 1:2], in1=neghalf, op=ALU.pow),
             [stt.res, pg.cres], [stt.res])
        s.op("dve", lambda e: e.scalar_tensor_tensor(
            out=y.ap, in0=x.ap, scalar=stt.ap[:, 2:3], in1=gain.ap, op0=ALU.mult, op1=ALU.mult),
            [x.res, stt.res, gain.res], [y.res])
        ystate[(t0, j)] = y

    def front_b(h, t0, j):
        y = ystate.pop((t0, j))
        for k in range(8):
            s.op("pe", lambda e, k=k: e.transpose(out=psb.ap[:, k * P:(k + 1) * P],
                                                  in_=y.ap[:, k * P:(k + 1) * P], identity=ident.ap),
                 [y.res, ident.res], [psb.res], signal=(k == 7))
        s.op("act", lambda e: e.copy(out=h.ap[:, :, j * P:(j + 1) * P],
                                     in_=psb.ap.rearrange("p (k t) -> p k t", k=8)),
             [psb.res], [h.res])

    h = hT.next()
    for j in range(sts[0][1]):
        front_a(h, sts[0][0], j)
        front_b(h, sts[0][0], j)
    for si, (t0, nt) in enumerate(sts):
        N = nt * P
        nxt = sts[si + 1] if si + 1 < len(sts) else None
        hn = hT.next() if nxt else None
        a = aT.next()
        for f in range(ncp):
            pg_ = ps_g.next()
            pu_ = ps_u.next()
            for k in range(8):
                s.op("pe", lambda e, pg_=pg_, k=k, f=f, h=h, N=N: e.matmul(
                    pg_.ap[:, 0:N], lhsT=wg_s[:, k, f * P:(f + 1) * P], rhs=h.ap[:, k, 0:N],
                    start=(k == 0), stop=(k == 7)),
                    [h.res, rg[k]], [pg_.res], signal=(k == 7))
            for k in range(8):
                s.op("pe", lambda e, pu_=pu_, k=k, f=f, h=h, N=N: e.matmul(
                    pu_.ap[:, 0:N], lhsT=wu_s[:, k, f * P:(f + 1) * P], rhs=h.ap[:, k, 0:N],
                    start=(k == 0), stop=(k == 7)),
                    [h.res, ru[k]], [pu_.res], signal=(k == 7))
            g_ = sg.next()
            s.op("act", lambda e, g_=g_, pg_=pg_, N=N: e.activation(out=g_.ap[:, 0:N], in_=pg_.ap[:, 0:N],
                                                             func=AF.Silu),
                 [pg_.res], [g_.res])
            s.op("dve", lambda e, f=f, g_=g_, pu_=pu_, a=a, N=N: e.tensor_tensor(
                out=a.ap[:, f, 0:N], in0=g_.ap[:, 0:N], in1=pu_.ap[:, 0:N], op=ALU.mult),
                [g_.res, pu_.res], [a.res])
            if nxt:
                jj = f // 2
                if jj < nxt[1]:
                    if f % 2 == 0:
                        front_a(hn, nxt[0], jj)
                    else:
                        front_b(hn, nxt[0], jj)
        if nxt:
            for (t0n, jn) in sorted(ystate.keys()):
                front_b(hn, t0n, jn)
            for jj in range((ncp + 1) // 2, nxt[1]):
                front_a(hn, nxt[0], jj)
                front_b(hn, nxt[0], jj)
        for j in range(nt):
            ti = t0 + j
            ac = at.next()
            s.dma("sp", lambda e, ac=ac, ti=ti: e.dma_start(out=ac.ap, in_=acc_src[ti * P:(ti + 1) * P, :]),
                  [pg.dr(accname, ti)], [ac.res])
            o = ot.next()
            for dh in range(2):
                po = ps_o.next()
                for f in range(ncp):
                    s.op("pe", lambda e, po=po, f=f, j=j, dh=dh, a=a: e.matmul(
                        po.ap, lhsT=a.ap[:, f, j * P:(j + 1) * P], rhs=wd_s[:, f, dh * 512:(dh + 1) * 512],
                        start=(f == 0), stop=(f == ncp - 1)),
                        [a.res, rd[f]], [po.res], signal=(f == ncp - 1))
                s.op("dve", lambda e, o=o, po=po, ac=ac, dh=dh: e.scalar_tensor_tensor(
                    out=o.ap[:, dh * 512:(dh + 1) * 512], in0=po.ap, scalar=0.5,
                    in1=ac.ap[:, dh * 512:(dh + 1) * 512], op0=ALU.mult, op1=ALU.add),
                    [po.res, ac.res], [o.res])
            s.dma("sp", lambda e, o=o, ti=ti: e.dma_start(out=dst[ti * P:(ti + 1) * P, :], in_=o.ap),
                  [o.res], [pg.dr(dstname, ti)])
        h = hn


FFN_PARTS = [(0, 8), (8, 15), (15, 22)]


def MM(pg, out, out_ap, lhsT, rhs, reads, start, stop, sig=None):
    if sig is None:
        sig = stop
    pg.s.op("pe", lambda e: e.matmul(out_ap, lhsT=lhsT, rhs=rhs, start=start, stop=stop),
            reads, [out.res], signal=sig)


def TR(pg, out, out_ap, in_ap, reads, sig=True):
    npart = in_ap.shape[0]
    idn = pg.ident
    pg.s.op("pe", lambda e: e.transpose(out=out_ap, in_=in_ap, identity=idn.ap[0:npart, 0:npart]),
            list(reads) + [idn.res], [out.res], signal=sig)


def ACT(pg, out_ap, in_ap, func, reads, writes, **kw):
    pg.s.op("act", lambda e: e.activation(out=out_ap, in_=in_ap, func=func, **kw), reads, writes)


def TT(pg, eng, out_ap, in0, in1, op, reads, writes):
    pg.s.op(eng, lambda e: e.tensor_tensor(out=out_ap, in0=in0, in1=in1, op=op), reads, writes)


def TS(pg, eng, out_ap, in0, s1, s2, op0, op1, reads, writes):
    if op1 is None:
        pg.s.op(eng, lambda e: e.tensor_scalar(out=out_ap, in0=in0, scalar1=s1, scalar2=None, op0=op0),
                reads, writes)
    else:
        pg.s.op(eng, lambda e: e.tensor_scalar(out=out_ap, in0=in0, scalar1=s1, scalar2=s2, op0=op0, op1=op1),
                reads, writes)


def STT(pg, out_ap, in0, scalar, in1, op0, op1, reads, writes):
    pg.s.op("dve", lambda e: e.scalar_tensor_tensor(out=out_ap, in0=in0, scalar=scalar, in1=in1,
                                                    op0=op0, op1=op1), reads, writes)


def LD(pg, out_tile, out_ap, in_ap, dres, queue="sp"):
    pg.s.dma(queue, lambda e: e.dma_start(out=out_ap, in_=in_ap), dres, [out_tile.res])


def ST(pg, out_ap, in_tile, in_ap, dres, queue="sp"):
    pg.s.dma(queue, lambda e: e.dma_start(out=out_ap, in_=in_ap), [in_tile.res], dres)


def load_w(pg, slot, w, c0, ncols, off, queue="pool"):
    ws = pg.wslot[slot]
    v = ws.ap[:, off:off + 8 * ncols].rearrange("p (k c) -> p k c", k=8)
    wv = w.rearrange("(k p) c -> p k c", p=P)
    rs = []
    for k in range(8):
        r = Res("w%d" % k)
        pg.s.dma(queue, lambda e, k=k: e.dma_start(out=v[:, k, :], in_=wv[:, k, c0:c0 + ncols]), [], [r])
        rs.append(r)
    return v, rs


class NormEnv:
    def __init__(self, pg, gain_dram, nyb=2):
        self.pg = pg
        self.gain = pg.tf(D, "gain")
        LD(pg, self.gain, self.gain.ap, gain_dram.partition_broadcast(P), [])
        self.yb = Ring([pg.tb(D, "yb%d" % i) for i in range(nyb)])
        self.st = Ring([pg.tf(4, "nst%d" % i) for i in range(4)])

    def run(self, x, h, c0):
        pg = self.pg
        y = self.yb.next()
        stt = self.st.next()
        ACT(pg, y.ap, x.ap, AF.Square, [x.res], [y.res, stt.res], accum_out=stt.ap[:, 0:1])
        TS(pg, "pool", stt.ap[:, 1:2], stt.ap[:, 0:1], 1.0 / D, EPS, ALU.mult, ALU.add, [stt.res], [stt.res])
        TT(pg, "pool", stt.ap[:, 2:3], stt.ap[:, 1:2], pg.neghalf, ALU.pow, [stt.res, pg.cres], [stt.res])
        STT(pg, y.ap, x.ap, stt.ap[:, 2:3], self.gain.ap, ALU.mult, ALU.mult, [x.res, stt.res, self.gain.res],
            [y.res])
        psb = pg.ps_bf
        for k in range(8):
            TR(pg, psb, psb.ap[:, k * P:(k + 1) * P], y.ap[:, k * P:(k + 1) * P], [y.res], sig=(k == 7))
        pg.s.op("act", lambda e: e.copy(out=h.ap[:, :, c0:c0 + P],
                                        in_=psb.ap.rearrange("p (k t) -> p k t", k=8)), [psb.res], [h.res])


def head_norm(pg, src_ap, src_reads, nh, gain_ap, gain_res, out_tile, out_ap, tmp, stt, hd=64, gain_full=False):
    n = nh * hd
    ACT(pg, tmp.ap[:, 0:n], src_ap, AF.Square, src_reads, [tmp.res])
    pg.s.op("dve", lambda e: e.tensor_reduce(out=stt.ap[:, 0:nh],
                                             in_=tmp.ap[:, 0:n].rearrange("p (h e) -> p h e", e=hd),
                                             axis=AX.X, op=ALU.add), [tmp.res], [stt.res])
    TS(pg, "pool", stt.ap[:, 0:nh], stt.ap[:, 0:nh], 1.0 / hd, EPS, ALU.mult, ALU.add, [stt.res], [stt.res])
    pg.s.op("pool", lambda e: e.tensor_tensor(out=stt.ap[:, nh:2 * nh], in0=stt.ap[:, 0:nh],
                                              in1=pg.neghalf.broadcast_to([P, nh]), op=ALU.pow),
            [stt.res, pg.cres], [stt.res])
    s3 = src_ap.rearrange("p (h e) -> p h e", e=hd)
    t3 = tmp.ap[:, 0:n].rearrange("p (h e) -> p h e", e=hd)
    o3 = out_ap.rearrange("p (h e) -> p h e", e=hd)
    TT(pg, "dve", t3, s3, stt.ap[:, nh:2 * nh].unsqueeze(2).broadcast_to([P, nh, hd]), ALU.mult,
       list(src_reads) + [stt.res], [tmp.res])
    g3 = gain_ap.rearrange("p (h e) -> p h e", e=hd) if gain_full else gain_ap.unsqueeze(1).broadcast_to([P, nh, hd])
    TT(pg, "dve", o3, t3, g3, ALU.mult, [tmp.res, gain_res], [out_tile.res])


MW = 768
DK = 192


def phase_inproj(pg, x1n, x1, gain_dram, wqk, rqk, wrest, rrest, conv_l, gbias, tri_d, ones_d, o):
    s = pg.s
    ne = NormEnv(pg, gain_dram)
    xt = Ring([pg.tf(D, "xt%d" % i) for i in range(2)])
    pre = Ring([pg.tf(515, "pre%d" % i) for i in range(2)])
    cv = Ring([pg.tf(512, "cv%d" % i) for i in range(2)])
    ktmp = pg.tf(512, "ktmp")
    carry = pg.tf(48, "carry", ("p (i c) -> p i c", dict(c=3)))
    convw = pg.tf(64, "convw", ("p (i c) -> p i c", dict(c=4)))
    gb = pg.tf(8, "gb")
    so = pg.tf(768, "so")
    qm = pg.tf(256, "qm")
    gt = pg.tf(24, "gt")
    gv = pg.tf(12, "gv")
    tri = pg.tf(P, "tri")
    ones = pg.tf(P, "ones")
    hT = pg.tb(8 * 512, "hT", ("p (k t) -> p k t", dict(k=8)))
    qko = Ring([pg.tb(512, "qko%d" % i) for i in range(2)])
    ktok = [pg.tb(768, "ktok%d" % i) for i in range(4)]
    vt = pg.tb(768, "vt")
    psb = pg.ps_bf
    ps = pg.ps
    LD(pg, convw, convw.ap[0:96], conv_l.rearrange("p (i c) -> p i c", c=4), [])
    LD(pg, gb, gb.ap, gbias.partition_broadcast(P), [])
    LD(pg, tri, tri.ap, tri_d, [])
    LD(pg, ones, ones.ap, ones_d, [])
    s.op("dve", lambda e: e.memset(carry.ap, 0.0), [], [carry.res])
    psr = Ring([ps[0], ps[1]])
    import os
    for si in range(-1, int(os.environ.get('NSI', NT // 4))):
        halo = si < 0
        nt = 1 if halo else 4
        N = nt * P
        tbase = 0 if halo else 1 + si * 4
        obase = si * 512
        for j in range(nt):
            x = xt.next()
            LD(pg, x, x.ap, x1[(tbase + j) * P:(tbase + j + 1) * P, :], [pg.dr(x1n, tbase + j)])
            ne.run(x, hT, j * P)
        for i in range(16):
            pp = psr.next()
            for k in range(8):
                MM(pg, pp, pp.ap[0:96, 0:N], wqk[:, k, i * 96:(i + 1) * 96], hT.ap[:, k, 0:N],
                   [hT.res, rqk[k]], k == 0, k == 7)
            pr = pre.next()
            s.op("dve", lambda e, pr=pr, i=i: e.tensor_copy(out=pr.ap[0:96, 0:3], in_=carry.ap[0:96, i, :]),
                 [carry.res], [pr.res])
            s.op("act", lambda e, pr=pr, pp=pp, N=N: e.copy(out=pr.ap[0:96, 3:3 + N], in_=pp.ap[0:96, 0:N]),
                 [pp.res], [pr.res])
            s.op("dve", lambda e, pr=pr, i=i, N=N: e.tensor_copy(out=carry.ap[0:96, i, :], in_=pr.ap[0:96, N:N + 3]),
                 [pr.res], [carry.res])
            if halo:
                continue
            c = cv.next()
            TS(pg, "dve", c.ap[0:96, 0:N], pr.ap[0:96, 0:N], convw.ap[0:96, i, 0:1], None, ALU.mult, None,
               [pr.res, convw.res], [c.res])
            for tp in range(1, 4):
                STT(pg, c.ap[0:96, 0:N], pr.ap[0:96, tp:tp + N], convw.ap[0:96, i, tp:tp + 1], c.ap[0:96, 0:N],
                    ALU.mult, ALU.add, [pr.res, convw.res, c.res], [c.res])
            qo = qko.next()
            if i < 8:
                ACT(pg, qo.ap[0:96, 0:N], c.ap[0:96, 0:N], AF.Silu, [c.res], [qo.res])
            else:
                ACT(pg, ktmp.ap[0:96, 0:N], c.ap[0:96, 0:N], AF.Silu, [c.res], [ktmp.res])
                TS(pg, "pool", qo.ap[0:96, 0:N], ktmp.ap[0:96, 0:N], DK ** -0.5, None, ALU.mult, None,
                   [ktmp.res], [qo.res])
            ST(pg, o["QKT"][i, :, obase:obase + N], qo, qo.ap[0:96, 0:N],
               [pg.dr("QKT", (i, si))])
            if i >= 8 and not os.environ.get('SKIPTR'):
                for j in range(nt):
                    TR(pg, psb, psb.ap[:, 0:96], qo.ap[0:96, j * P:(j + 1) * P], [qo.res])
                    s.op("act", lambda e, j=j, i=i: e.copy(out=ktok[j].ap[:, (i - 8) * 96:(i - 7) * 96],
                                                           in_=psb.ap[:, 0:96]), [psb.res], [ktok[j].res])
        if halo:
            continue
        for j in range(nt):
            r0 = obase + j * P
            ST(pg, o["KTOK"][r0:r0 + P, :], ktok[j], ktok[j].ap, [pg.dr("KTOK", r0)])
            if os.environ.get('SKIP2'):
                continue
            banks = [ps[2], ps[3], ps[4], ps[5]]
            rng = [(0, 512), (512, 1024), (1024, 1536), (1536, 1800)]
            for bi, (a0, a1) in enumerate(rng):
                for k in range(8):
                    MM(pg, banks[bi], banks[bi].ap[:, 0:a1 - a0], hT.ap[:, k, j * P:(j + 1) * P],
                       wrest[:, k, a0:a1], [hT.res, rrest[k]], k == 0, k == 7)
            s.op("act", lambda e: e.copy(out=vt.ap[:, 0:512], in_=ps[2].ap), [ps[2].res], [vt.res])
            s.op("act", lambda e: e.copy(out=vt.ap[:, 512:768], in_=ps[3].ap[:, 0:256]), [ps[3].res], [vt.res])
            ST(pg, o["VTOK"][r0:r0 + P, :], vt, vt.ap, [pg.dr("VTOK", r0)])
            ACT(pg, so.ap[:, 0:256], ps[3].ap[:, 256:512], AF.Sigmoid, [ps[3].res], [so.res])
            ACT(pg, so.ap[:, 256:768], ps[4].ap, AF.Sigmoid, [ps[4].res], [so.res])
            ST(pg, o["SIGO"][r0:r0 + P, :], so, so.ap, [pg.dr("SIGO", r0)])
            s.op("act", lambda e: e.copy(out=qm.ap, in_=ps[5].ap[:, 8:264]), [ps[5].res], [qm.res])
            s.op("act", lambda e: e.copy(out=gt.ap[:, 0:8], in_=ps[5].ap[:, 0:8]), [ps[5].res], [gt.res])
            ST(pg, o["QMEM"][r0:r0 + P, :], qm, qm.ap, [pg.dr("QMEM", r0)])
            TT(pg, "dve", gt.ap[:, 0:8], gt.ap[:, 0:8], gb.ap, ALU.add, [gt.res, gb.res], [gt.res])
            ACT(pg, gt.ap[:, 8:12], gt.ap[:, 4:8], AF.Exp, [gt.res], [gt.res], scale=-1.0)
            TS(pg, "dve", gt.ap[:, 8:12], gt.ap[:, 8:12], 1.0, None, ALU.add, None, [gt.res], [gt.res])
            ACT(pg, gt.ap[:, 8:12], gt.ap[:, 8:12], AF.Ln, [gt.res], [gt.res])
            pq = ps[6]
            MM(pg, pq, pq.ap[:, 0:4], tri.ap, gt.ap[:, 8:12], [tri.res, gt.res], True, True)
            MM(pg, pq, pq.ap[:, 4:8], ones.ap, gt.ap[:, 8:12], [ones.res, gt.res], True, True)
            s.op("act", lambda e: e.copy(out=gt.ap[:, 16:24], in_=pq.ap[:, 0:8]), [pq.res], [gt.res])
            TT(pg, "dve", gt.ap[:, 12:16], gt.ap[:, 0:4], gt.ap[:, 16:20], ALU.add, [gt.res], [gt.res])
            ACT(pg, gv.ap[:, 0:4], gt.ap[:, 12:16], AF.Exp, [gt.res], [gv.res])
            ACT(pg, gv.ap[:, 4:8], gt.ap[:, 16:20], AF.Exp, [gt.res], [gv.res])
            ACT(pg, gv.ap[:, 8:12], gt.ap[:, 20:24], AF.Exp, [gt.res], [gv.res], scale=-1.0)
            ST(pg, o["GV"][r0:r0 + P, :], gv, gv.ap, [pg.dr("GV", r0)])


ATT_SCALE = 64 ** -0.5


def phase_memattn(pg, mem, memg, wkv, rkv, mkg, mqg, qmem, hm):
    s = pg.s
    ne = NormEnv(pg, memg)
    xt = Ring([pg.tf(D, "xt%d" % i) for i in range(2)])
    tmp = pg.tf(256, "tmp")
    stt = pg.tf(8, "stt")
    kg = pg.tf(64, "kg")
    qg = pg.tf(64, "qg")
    rec = pg.tf(4, "rec")
    qm = Ring([pg.tf(256, "qm%d" % i) for i in range(2)])
    hT = pg.tb(8 * 256, "hT", ("p (k t) -> p k t", dict(k=8)))
    kn = pg.tb(256, "kn")
    knT = pg.tb(2 * 256, "knT", ("p (c m) -> p c m", dict(c=2)))
    vx = pg.tb(2 * 4 * 65, "vx", ("p (m h e) -> p m h e", dict(m=2, h=4)))
    qn = pg.tb(256, "qn")
    qnT = pg.tb(256, "qnT", ("p (c t) -> p c t", dict(c=2)))
    pt = pg.tb(1024, "pt", ("p (g t) -> p g t", dict(g=8)))
    ho = Ring([pg.tb(256, "ho%d" % i) for i in range(2)])
    psb = pg.ps_bf
    ps = pg.ps
    LD(pg, kg, kg.ap, mkg.partition_broadcast(P), [])
    LD(pg, qg, qg.ap, mqg.partition_broadcast(P), [])
    s.op("dve", lambda e: e.memset(vx.ap, 1.0), [], [vx.res])
    for mt in range(2):
        x = xt.next()
        LD(pg, x, x.ap, mem[mt * P:(mt + 1) * P, :], [])
        ne.run(x, hT, mt * P)
    for mt in range(2):
        pk = ps[0]
        for k in range(8):
            MM(pg, pk, pk.ap, hT.ap[:, k, mt * P:(mt + 1) * P], wkv[:, k, :], [hT.res, rkv[k]], k == 0, k == 7)
        head_norm(pg, pk.ap[:, 0:256], [pk.res], 4, kg.ap, kg.res, kn, kn.ap, tmp, stt)
        s.op("act", lambda e, mt=mt: e.copy(out=vx.ap[:, mt, :, 0:64],
                                            in_=pk.ap[:, 256:512].rearrange("p (h e) -> p h e", e=64)),
             [pk.res], [vx.res])
        for c in range(2):
            TR(pg, psb, psb.ap[:, c * P:(c + 1) * P], kn.ap[:, c * P:(c + 1) * P], [kn.res], sig=(c == 1))
        s.op("act", lambda e, mt=mt: e.copy(out=knT.ap[:, :, mt * P:(mt + 1) * P],
                                            in_=psb.ap[:, 0:256].rearrange("p (c t) -> p c t", c=2)),
             [psb.res], [knT.res])
    import os
    for ti in range(int(os.environ.get('NMT', NT))):
        q = qm.next()
        LD(pg, q, q.ap, qmem[ti * P:(ti + 1) * P, :], [pg.dr("QMEM", ti * P)])
        head_norm(pg, q.ap, [q.res], 4, qg.ap, qg.res, qn, qn.ap, tmp, stt)
        for c in range(2):
            TR(pg, psb, psb.ap[:, c * P:(c + 1) * P], qn.ap[:, c * P:(c + 1) * P], [qn.res], sig=(c == 1))
        s.op("act", lambda e: e.copy(out=qnT.ap, in_=psb.ap[:, 0:256].rearrange("p (c t) -> p c t", c=2)),
             [psb.res], [qnT.res])
        for h in range(4):
            c, b0 = h // 2, 64 * (h % 2)
            bank = ps[1 + h % 2]
            for mt in range(2):
                col = ((h // 2) * 2 + mt) * P
                MM(pg, bank, bank.ap[:, col:col + P], knT.ap[b0:b0 + 64, c, mt * P:(mt + 1) * P],
                   qnT.ap[b0:b0 + 64, c, :], [knT.res, qnT.res], True, True)
        for hb in range(2):
            bank = ps[1 + hb]
            ACT(pg, pt.ap[:, hb * 4:(hb + 1) * 4, :], bank.ap.rearrange("p (g t) -> p g t", g=4), AF.Exp,
                [bank.res], [pt.res], scale=ATT_SCALE)
        po = ps[3]
        for h in range(4):
            for mt in range(2):
                MM(pg, po, po.ap[:, h * 65:(h + 1) * 65], pt.ap[:, (h % 2) * 4 + (h // 2) * 2 + mt, :], vx.ap[:, mt, h, :],
                   [pt.res, vx.res], mt == 0, mt == 1)
        po3 = po.ap[:, 0:260].rearrange("p (h e) -> p h e", e=65)
        s.op("dve", lambda e: e.reciprocal(out=rec.ap, in_=po3[:, :, 64]), [po.res], [rec.res])
        hh = ho.next()
        TT(pg, "dve", hh.ap.rearrange("p (h e) -> p h e", e=64), po3[:, :, 0:64],
           rec.ap.unsqueeze(2).broadcast_to([P, 4, 64]), ALU.mult, [po.res, rec.res], [hh.res])
        ST(pg, hm[ti * P:(ti + 1) * P, :], hh, hh.ap, [pg.dr("HM", ti)])


def ffn_full(pg, wg, wu, wd, gain_dram, srcn, src, dstn, dst, ntiles, first_loaded=None, after=None):
    wv = first_loaded or load_ffn_weights(pg, 0, wg, wu, wd, *FFN_PARTS[0])
    ret = None
    for p, (c0, c1) in enumerate(FFN_PARTS):
        pg.phase()
        if p + 1 < len(FFN_PARTS):
            wvn = load_ffn_weights(pg, (p + 1) % 2, wg, wu, wd, *FFN_PARTS[p + 1])
        elif after is not None:
            ret = after()
        ffn_pass(pg, srcn, src, srcn if p == 0 else dstn, src if p == 0 else dst, dstn, dst, wv, c1 - c0,
                 gain_dram, ntiles)
        wv = wvn
    return ret


def common_inputs(pg):
    c = {}
    c["idn"] = pg.dt("idn", [P, P], BF16, "ExternalInput")
    c["tri"] = pg.dt("tri", [P, P], F32, "ExternalInput")
    c["ones"] = pg.dt("ones", [P, P], F32, "ExternalInput")
    return c


def build_L1():
    pg = Prog()
    c = common_inputs(pg)
    xin = pg.dt("x_ext", [(NT + 1) * P, D], F32, "ExternalInput")
    mem = pg.dt("mem", [256, D], F32, "ExternalInput")
    wg = pg.dt("wg", [D, FF], F32, "ExternalInput")
    wu = pg.dt("wu", [D, FF], F32, "ExternalInput")
    wd = pg.dt("wd", [FF, D], F32, "ExternalInput")
    g1 = pg.dt("g_ffn", [1, D], F32, "ExternalInput")
    gmix = pg.dt("g_mix", [1, D], F32, "ExternalInput")
    gmem = pg.dt("g_mem", [1, D], F32, "ExternalInput")
    w_in = pg.dt("w_in", [D, 3336], F32, "ExternalInput")
    wkv = pg.dt("w_mkv", [D, 512], F32, "ExternalInput")
    mkg = pg.dt("mkg", [1, 64], F32, "ExternalInput")
    mqg = pg.dt("mqg", [1, 64], F32, "ExternalInput")
    conv_l = pg.dt("conv_l", [96, 64], F32, "ExternalInput")
    gbias = pg.dt("gbias", [1, 8], F32, "ExternalInput")
    o = {}
    x1 = pg.dt("x1", [(NT + 1) * P, D], F32, "ExternalOutput")
    o["QKT"] = pg.dt("QKT", [16, 96, TOK], BF16, "ExternalOutput")
    o["KTOK"] = pg.dt("KTOK", [TOK, 768], BF16, "ExternalOutput")
    o["VTOK"] = pg.dt("VTOK", [TOK, 768], BF16, "ExternalOutput")
    o["SIGO"] = pg.dt("SIGO", [TOK, 768], F32, "ExternalOutput")
    o["GV"] = pg.dt("GV", [TOK, 12], F32, "ExternalOutput")
    o["QMEM"] = pg.dt("QMEM", [TOK, 256], F32, "Internal")
    hm = pg.dt("HM", [TOK, 256], BF16, "ExternalOutput")
    setup_consts(pg, c["idn"])
    import os
    if os.environ.get("L1NOFFN"):
        wqk = load_w(pg, 1, w_in, 0, 1536, 0)
        x1 = xin
    else:
        wqk = ffn_full(pg, wg, wu, wd, g1, "x_ext", xin, "x1", x1, NT + 1,
                       after=lambda: load_w(pg, 1, w_in, 0, 1536, 0))
    import os
    stop = int(os.environ.get("L1STOP", "9"))
    if stop < 1:
        return pg.finish()
    pg.phase()
    wrest = load_w(pg, 0, w_in, 1536, 1800, 0)
    phase_inproj(pg, "x1", x1, gmix, wqk[0], wqk[1], wrest[0], wrest[1], conv_l, gbias, c["tri"], c["ones"], o)
    if stop < 2:
        return pg.finish()
    pg.phase()
    wk = load_w(pg, 1, wkv, 0, 512, 0)
    phase_memattn(pg, mem, gmem, wk[0], wk[1], mkg, mqg, o["QMEM"], hm)
    return pg.finish()


def _consts():
    import ml_dtypes
    idn = np.eye(P, dtype=np.float32).astype(ml_dtypes.bfloat16)
    tri = np.triu(np.ones((P, P), dtype=np.float32))
    ones = np.ones((P, P), dtype=np.float32)
    return dict(idn=idn, tri=tri, ones=ones)


def core_tokens(c):
    return c // 4, (c % 4) * TOK


def run_L1(inp, cores):
    nc = build_L1()
    cs = _consts()
    conv = inp["a_conv"][0]
    conv_l = np.ascontiguousarray(conv.T.reshape(16, 96, 4).transpose(1, 0, 2).reshape(96, 64))
    maps = []
    for c in cores:
        b, t0 = core_tokens(c)
        xe = np.zeros(((NT + 1) * P, D), np.float32)
        xe[P:] = inp["x"][b, t0:t0 + TOK]
        if t0 > 0:
            xe[:P] = inp["x"][b, t0 - P:t0]
        maps.append(dict(cs, x_ext=xe, mem=np.ascontiguousarray(inp["mem"][b]),
                         wg=inp["ffn1_w_gate"][0], wu=inp["ffn1_w_up"][0], wd=inp["ffn1_w_down"][0],
                         g_ffn=inp["ffn1_norm"][0:1], g_mix=inp["mix_norm"][0:1], g_mem=inp["mem_norm"][0:1],
                         w_in=inp["a_w_in"][0], w_mkv=inp["w_mem_kv"][0],
                         mkg=inp["mem_k_norm"][0:1], mqg=inp["mem_q_norm"][0:1],
                         conv_l=conv_l, gbias=inp["a_gate_bias"][0:1]))
    maps = [{k: np.ascontiguousarray(v) for k, v in m.items()} for m in maps]
    return run_bass_kernel_spmd(nc, maps, core_ids=list(range(len(cores)))).results


def phase_outproj(pg, ntok_tiles, A_fill, xname, x_src, x_row0, wout, rout, dstname, dst):
    s = pg.s
    A = Ring([pg.tb(D, "A%d" % i) for i in range(2)])
    AT = Ring([pg.tb(D, "AT%d" % i, ("p (c t) -> p c t", dict(c=8))) for i in range(2)])
    xt = Ring([pg.tf(D, "oxt%d" % i) for i in range(2)])
    ot = Ring([pg.tf(D, "oot%d" % i) for i in range(2)])
    psb = pg.ps_bf
    pso = Ring([pg.ps[5], pg.ps[6]])
    for ti in range(ntok_tiles):
        a = A.next()
        A_fill(ti, a)
        x = xt.next()
        LD(pg, x, x.ap, x_src[x_row0 + ti * P:x_row0 + (ti + 1) * P, :], [pg.dr(xname, ti)])
        for c in range(8):
            TR(pg, psb, psb.ap[:, c * P:(c + 1) * P], a.ap[:, c * P:(c + 1) * P], [a.res], sig=(c == 7))
        at = AT.next()
        s.op("act", lambda e, at=at: e.copy(out=at.ap, in_=psb.ap.rearrange("p (c t) -> p c t", c=8)),
             [psb.res], [at.res])
        o = ot.next()
        for dh in range(2):
            po = pso.next()
            for c in range(8):
                MM(pg, po, po.ap, at.ap[:, c, :], wout[:, c, dh * 512:(dh + 1) * 512], [at.res, rout[c]],
                   c == 0, c == 7)
            TT(pg, "dve", o.ap[:, dh * 512:(dh + 1) * 512], po.ap, x.ap[:, dh * 512:(dh + 1) * 512], ALU.add,
               [po.res, x.res], [o.res])
        ST(pg, dst[ti * P:(ti + 1) * P, :], o, o.ap, [pg.dr(dstname, ti)])


def phase_mlstm_out(pg, cell_d, sigo_d, hm_d, hgain_d, x1, wout, rout, x2):
    hg = pg.tf(768, "hg")
    LD(pg, hg, hg.ap, hgain_d.partition_broadcast(P), [])
    ce = Ring([pg.tf(768, "ce%d" % i) for i in range(2)])
    so = Ring([pg.tf(768, "so%d" % i) for i in range(2)])
    tmp = pg.tf(768, "tmp")
    tmp2 = pg.tf(768, "tmp2")
    stt = pg.tf(8, "stt")

    def fill(ti, a):
        c = ce.next()
        g = so.next()
        LD(pg, c, c.ap, cell_d[ti * P:(ti + 1) * P, :], [])
        LD(pg, g, g.ap, sigo_d[ti * P:(ti + 1) * P, :], [])
        LD(pg, a, a.ap[:, 768:1024], hm_d[ti * P:(ti + 1) * P, :], [])
        head_norm(pg, c.ap, [c.res], 4, hg.ap, hg.res, tmp2, tmp2.ap, tmp, stt, hd=192, gain_full=True)
        TT(pg, "dve", a.ap[:, 0:768], tmp2.ap, g.ap, ALU.mult, [tmp2.res, g.res], [a.res])
    phase_outproj(pg, NT, fill, "x1", x1, P, wout, rout, "x2", x2)


def phase_projnorm(pg, xname, x_src, gain_dram, wv, wr, nbanks, hgain_d, outT, second, second_out):
    s = pg.s
    ne = NormEnv(pg, gain_dram)
    xt = Ring([pg.tf(D, "xt%d" % i) for i in range(2)])
    tmp = pg.tf(512, "tmp")
    stt = pg.tf(16, "stt")
    hgn = pg.tf(64, "hgn")
    LD(pg, hgn, hgn.ap, hgain_d.partition_broadcast(P), [])
    hT = pg.tb(8 * P, "hT", ("p (k t) -> p k t", dict(k=8)))
    kn = Ring([pg.tb(768, "kn%d" % i) for i in range(2)])
    kT = Ring([pg.tb(768, "kT%d" % i, ("p (c t) -> p c t", dict(c=6))) for i in range(2)])
    if second == "v":
        sec = Ring([pg.tb(768, "sv%d" % i) for i in range(2)])
    else:
        sec = Ring([pg.tf(256, "sq%d" % i) for i in range(2)])
    psb = pg.ps_bf
    banks = [pg.ps[i] for i in range(nbanks)]
    outT_v = outT.rearrange("(c p) t -> p c t", p=P)
    for ti in range(NT):
        x = xt.next()
        LD(pg, x, x.ap, x_src[ti * P:(ti + 1) * P, :], [pg.dr(xname, ti)])
        ne.run(x, hT, 0)
        for b in range(nbanks):
            for k in range(8):
                MM(pg, banks[b], banks[b].ap, hT.ap[:, k, :], wv[:, k, b * 512:(b + 1) * 512], [hT.res, wr[k]],
                   k == 0, k == 7)
        n = kn.next()
        head_norm(pg, banks[0].ap, [banks[0].res], 8, hgn.ap, hgn.res, n, n.ap[:, 0:512], tmp, stt)
        head_norm(pg, banks[1].ap[:, 0:256], [banks[1].res], 4, hgn.ap, hgn.res, n, n.ap[:, 512:768], tmp, stt)
        sc = sec.next()
        if second == "v":
            s.op("act", lambda e, sc=sc: e.copy(out=sc.ap[:, 0:256], in_=banks[1].ap[:, 256:512]),
                 [banks[1].res], [sc.res])
            s.op("act", lambda e, sc=sc: e.copy(out=sc.ap[:, 256:768], in_=banks[2].ap), [banks[2].res], [sc.res])
        else:
            s.op("act", lambda e, sc=sc: e.copy(out=sc.ap, in_=banks[1].ap[:, 256:512]), [banks[1].res], [sc.res])
        ST(pg, second_out[ti * P:(ti + 1) * P, :], sc, sc.ap, [pg.dr("sec", ti)])
        for c in range(6):
            TR(pg, psb, psb.ap[:, c * P:(c + 1) * P], n.ap[:, c * P:(c + 1) * P], [n.res], sig=(c == 5))
        kt = kT.next()
        s.op("act", lambda e, kt=kt: e.copy(out=kt.ap, in_=psb.ap[:, 0:768].rearrange("p (c t) -> p c t", c=6)),
             [psb.res], [kt.res])
        ST(pg, outT_v[:, :, ti * P:(ti + 1) * P], kt, kt.ap, [pg.dr("outT", ti)])


def build_L3():
    pg = Prog()
    c = common_inputs(pg)
    cell = pg.dt("cell", [TOK, 768], F32, "ExternalInput")
    sigo = pg.dt("SIGO", [TOK, 768], F32, "ExternalInput")
    hm0 = pg.dt("HM0", [TOK, 256], BF16, "ExternalInput")
    x1 = pg.dt("x1", [(NT + 1) * P, D], F32, "ExternalInput")
    mem = pg.dt("mem", [256, D], F32, "ExternalInput")
    hgain = pg.dt("hgain", [1, 768], F32, "ExternalInput")
    w_out = pg.dt("w_out", [D, D], F32, "ExternalInput")
    wg2 = pg.dt("wg2", [D, FF], F32, "ExternalInput")
    wu2 = pg.dt("wu2", [D, FF], F32, "ExternalInput")
    wd2 = pg.dt("wd2", [FF, D], F32, "ExternalInput")
    g2 = pg.dt("g_ffn2", [1, D], F32, "ExternalInput")
    gkv = pg.dt("g_kv", [1, D], F32, "ExternalInput")
    w_kv = pg.dt("w_kv", [D, 1536], F32, "ExternalInput")
    kkg = pg.dt("kkg", [1, 64], F32, "ExternalInput")
    wg1 = pg.dt("wg1", [D, FF], F32, "ExternalInput")
    wu1 = pg.dt("wu1", [D, FF], F32, "ExternalInput")
    wd1 = pg.dt("wd1", [FF, D], F32, "ExternalInput")
    g1 = pg.dt("g_ffn1", [1, D], F32, "ExternalInput")
    gmix = pg.dt("g_mix", [1, D], F32, "ExternalInput")
    w_q = pg.dt("w_q", [D, D], F32, "ExternalInput")
    qg = pg.dt("qg", [1, 64], F32, "ExternalInput")
    gmem = pg.dt("g_mem", [1, D], F32, "ExternalInput")
    wmkv = pg.dt("w_mkv", [D, 512], F32, "ExternalInput")
    mkg = pg.dt("mkg", [1, 64], F32, "ExternalInput")
    mqg = pg.dt("mqg", [1, 64], F32, "ExternalInput")
    x2 = pg.dt("x2", [TOK, D], F32, "Internal")
    x3 = pg.dt("x3", [TOK, D], F32, "ExternalOutput")
    x4 = pg.dt("x4", [TOK, D], F32, "ExternalOutput")
    KTD = pg.dt("KTD", [768, TOK], BF16, "ExternalOutput")
    VD = pg.dt("VD", [TOK, 768], BF16, "ExternalOutput")
    QTD = pg.dt("QTD", [768, TOK], BF16, "ExternalOutput")
    QMEM = pg.dt("QMEM1", [TOK, 256], F32, "Internal")
    HM1 = pg.dt("HM1", [TOK, 256], BF16, "ExternalOutput")
    setup_consts(pg, c["idn"])
    pg.phase()
    wo = load_w(pg, 1, w_out, 0, D, 0)
    ffw = load_ffn_weights(pg, 0, wg2, wu2, wd2, *FFN_PARTS[0])
    phase_mlstm_out(pg, cell, sigo, hm0, hgain, x1, wo[0], wo[1], x2)
    wkv = ffn_full(pg, wg2, wu2, wd2, g2, "x2", x2, "x3", x3, NT, first_loaded=ffw,
                   after=lambda: load_w(pg, 1, w_kv, 0, 1536, 0))
    pg.phase()
    ffw = load_ffn_weights(pg, 0, wg1, wu1, wd1, *FFN_PARTS[0])
    phase_projnorm(pg, "x3", x3, gkv, wkv[0], wkv[1], 3, kkg, KTD, "v", VD)
    wq = ffn_full(pg, wg1, wu1, wd1, g1, "x3", x3, "x4", x4, NT, first_loaded=ffw,
                  after=lambda: load_w(pg, 1, w_q, 0, D, 0))
    pg.phase()
    wk = load_w(pg, 0, wmkv, 0, 512, 0)
    phase_projnorm(pg, "x4", x4, gmix, wq[0], wq[1], 2, qg, QTD, "q", QMEM)
    pg.phase()
    phase_memattn(pg, mem, gmem, wk[0], wk[1], mkg, mqg, QMEM, HM1)
    return pg.finish()


def run_L3(inp, l1, l2, cores=range(NCORE)):
    nc = build_L3()
    cs = _consts()
    maps = []
    for c in cores:
        b, t0 = core_tokens(c)
        cell = np.concatenate([np.asarray(l2[b * 4 + h]["cell"])[t0:t0 + TOK] for h in range(4)], axis=1)
        r1 = l1[c]
        maps.append(dict(cs, cell=cell, SIGO=r1["SIGO"], HM0=r1["HM"], x1=r1["x1"], mem=inp["mem"][b],
                         hgain=inp["a_h_norm"][0].reshape(1, 768), w_out=inp["a_w_out"][0],
                         wg2=inp["ffn2_w_gate"][0], wu2=inp["ffn2_w_up"][0], wd2=inp["ffn2_w_down"][0],
                         g_ffn2=inp["ffn2_norm"][0:1], g_kv=inp["kv_norm"].reshape(1, D), w_kv=inp["w_kv"],
                         kkg=inp["kv_k_norm"].reshape(1, 64),
                         wg1=inp["ffn1_w_gate"][1], wu1=inp["ffn1_w_up"][1], wd1=inp["ffn1_w_down"][1],
                         g_ffn1=inp["ffn1_norm"][1:2], g_mix=inp["mix_norm"][1:2], w_q=inp["b_w_q"][0],
                         qg=inp["b_q_norm"][0:1], g_mem=inp["mem_norm"][1:2], w_mkv=inp["w_mem_kv"][1],
                         mkg=inp["mem_k_norm"][1:2], mqg=inp["mem_q_norm"][1:2]))
    maps = [{k: np.ascontiguousarray(v) for k, v in m.items()} for m in maps]
    return run_bass_kernel_spmd(nc, maps, core_ids=list(range(len(maps)))).results


DIL = (1, 4, 16)
HALO = 2048
NEG = -30000.0


def _bias_onehot():
    oh = np.zeros((3, 64, 384), np.float32)
    for g, d in enumerate(DIL):
        for j in range(384):
            m = j - 128
            if m < 0 or m > 128:
                oh[g, 32, j] = 1.0
                continue
            dist = np.int32(m * d)
            if dist < 16:
                bkt = int(dist)
            else:
                d_f = np.float32(max(int(dist), 1))
                val = np.log(d_f / np.float32(16)) / np.float32(np.log(2048 / 16)) * np.float32(16)
                bkt = min(16 + int(np.float32(val)), 31)
            oh[g, bkt, j] = 1.0
    return np.ascontiguousarray(oh[:, :, ::-1])


def phase_dilated(pg, KTX, VDX, QTD, relb, oh_d, flag_d, jrev_d, BV, NUMX):
    s = pg.s
    tab = pg.tf(12, "tab")
    oh = pg.tf(384, "oh")
    bv = pg.tf(384, "bv")
    flag = pg.tf(1, "flag")
    T32 = pg.tf(1024, "T32", ("p (a q) -> p a q", dict(a=8)))
    G32 = pg.tf(1024, "G32", ("p (a q) -> p a q", dict(a=8)))
    jrev = pg.tf(P, "jrev")
    LD(pg, jrev, jrev.ap, jrev_d, [])
    e32 = Ring([pg.tf(512, "e32_%d" % i, ("p (a q) -> p a q", dict(a=4))) for i in range(2)])
    osb = Ring([pg.tf(260, "osb%d" % i) for i in range(2)])
    vx = Ring([pg.tb(260, "vx%d" % i, ("p (h e) -> p h e", dict(h=4))) for i in range(3)])
    vxh = pg.tb(260, "vxh", ("p (h e) -> p h e", dict(h=4)))
    ptb = Ring([pg.tb(1024, "ptb%d" % i, ("p (a q) -> p a q", dict(a=8))) for i in range(2)])
    kt = Tile(pg.wslot[0].ap[:, 0:2 * (HALO + TOK)].rearrange("p (c t) -> p c t", c=2), "kt")
    qt = Tile(pg.wslot[1].ap[:, 0:2 * TOK].rearrange("p (c t) -> p c t", c=2), "qt")
    ps = pg.ps
    psA = Ring([ps[0], ps[2]])
    psB = Ring([ps[1], ps[3]])
    pso = Ring([ps[4], ps[5]])
    LD(pg, flag, flag.ap, flag_d, [])
    s.op("dve", lambda e: e.memset(tab.ap, 0.0), [], [tab.res])
    s.op("dve", lambda e: e.memset(tab.ap[32:33, :], NEG), [tab.res], [tab.res])
    LD(pg, tab, tab.ap[0:32, :], relb, [])
    for t in vx.items:
        s.op("pool", lambda e, t=t: e.memset(t.ap, 1.0), [], [t.res])
    s.op("pool", lambda e: e.tensor_copy(out=vxh.ap[:, :, 64], in_=flag.ap[:, 0:1].broadcast_to([P, 4])),
         [flag.res], [vxh.res])
    KTv = KTX.rearrange("(c p) t -> p c t", p=P)
    QTv = QTD.rearrange("(c p) t -> p c t", p=P)
    for g, d in enumerate(DIL):
        nb = TOK // (P * d)
        LD(pg, oh, oh.ap[0:64, :], oh_d[g], [])
        pb = ps[6]
        MM(pg, pb, pb.ap[0:12, 0:384], tab.ap[0:64, 0:12], oh.ap[0:64, :], [tab.res, oh.res], True, True)
        s.op("act", lambda e: e.copy(out=bv.ap[0:12, :], in_=pb.ap[0:12, 0:384]), [pb.res], [bv.res])
        ST(pg, BV[g], bv, bv.ap[0:12, :], [pg.dr("BV", g)])
        for hd in range(4):
            for kb in range(2):
                a = (hd % 2) * 4 + (hd // 2) * 2 + kb
                off = 256 if kb == 0 else 128
                base = (g * 12 + 4 * g + hd) * 384 + 256 - off
                src_ap = bass.AP(BV.tensor, base, [[1, P], [1, P]])
                LD(pg, G32, G32.ap[:, a, :], src_ap, [pg.dr("BV", g)])
        for a in range(8):
            bank = ps[6] if a < 4 else ps[0]
            MM(pg, bank, bank.ap[:, (a % 4) * P:(a % 4 + 1) * P], G32.ap[:, a, :], jrev.ap, [G32.res, jrev.res],
               True, True)
        for hb, bank in enumerate((ps[6], ps[0])):
            ACT(pg, T32.ap[:, hb * 4:(hb + 1) * 4, :], bank.ap.rearrange("p (a q) -> p a q", a=4), AF.Exp,
                [bank.res], [T32.res])
        LD(pg, kt, kt.ap, KTv[:, 2 * g:2 * g + 2, :], [])
        LD(pg, qt, qt.ap, QTv[:, 2 * g:2 * g + 2, :], [])
        for r in range(d):
            prev = vxh
            e0 = HALO + r - P * d
            LD(pg, vxh, vxh.ap[:, :, 0:64],
               VDX[e0:e0 + (P - 1) * d + 1:d, 256 * g:256 * g + 256].rearrange("p (h e) -> p h e", e=64), [])
            for n in range(nb):
                p0 = r + d * n * P
                e0 = HALO + p0
                cur = vx.next()
                LD(pg, cur, cur.ap[:, :, 0:64],
                   VDX[e0:e0 + (P - 1) * d + 1:d, 256 * g:256 * g + 256].rearrange("p (h e) -> p h e", e=64), [])
                bA, bB = psA.next(), psB.next()
                for hd in range(4):
                    ch, b0 = hd // 2, 64 * (hd % 2)
                    bank = bA if hd % 2 == 0 else bB
                    for kb in range(2):
                        ks = e0 - (P * d if kb == 0 else 0)
                        col = ((hd // 2) * 2 + kb) * P
                        MM(pg, bank, bank.ap[:, col:col + P], kt.ap[b0:b0 + 64, ch, ks:ks + (P - 1) * d + 1:d],
                           qt.ap[b0:b0 + 64, ch, p0:p0 + (P - 1) * d + 1:d], [kt.res, qt.res], True, True)
                pt = ptb.next()
                for hb, bank in enumerate((bA, bB)):
                    ee = e32.next()
                    ACT(pg, ee.ap, bank.ap.rearrange("p (a q) -> p a q", a=4), AF.Exp, [bank.res], [ee.res],
                        scale=ATT_SCALE)
                    TT(pg, "dve", pt.ap[:, hb * 4:(hb + 1) * 4, :], ee.ap, T32.ap[:, hb * 4:(hb + 1) * 4, :], ALU.mult,
                       [ee.res, T32.res], [pt.res])
                po = pso.next()
                for hd in range(4):
                    for kb in range(2):
                        a = (hd % 2) * 4 + (hd // 2) * 2 + kb
                        vt_ = prev if kb == 0 else cur
                        MM(pg, po, po.ap[:, hd * 65:(hd + 1) * 65], pt.ap[:, a, :], vt_.ap[:, hd, :],
                           [pt.res, vt_.res], kb == 0, kb == 1)
                ob = osb.next()
                s.op("act", lambda e, ob=ob, po=po: e.copy(out=ob.ap, in_=po.ap[:, 0:260]), [po.res], [ob.res])
                ST(pg, NUMX[p0:p0 + (P - 1) * d + 1:d, g * 260:(g + 1) * 260], ob, ob.ap, [pg.dr("NUMX", (g, r, n))])
                prev = cur


def phase_dil_out(pg, NUMX, hm_d, x4, wout, rout, x5):
    nx = Ring([pg.tf(780, "nx%d" % i) for i in range(2)])
    ds_ = pg.tf(8, "ds")

    def fill(ti, a):
        n = nx.next()
        LD(pg, n, n.ap, NUMX[ti * P:(ti + 1) * P, :], [])
        LD(pg, a, a.ap[:, 768:1024], hm_d[ti * P:(ti + 1) * P, :], [])
        n4 = n.ap.rearrange("p (g h e) -> p g h e", g=3, h=4)
        TT(pg, "dve", ds_.ap[:, 0:4], n4[:, 0, :, 64], n4[:, 1, :, 64], ALU.add, [n.res], [ds_.res])
        TT(pg, "dve", ds_.ap[:, 0:4], ds_.ap[:, 0:4], n4[:, 2, :, 64], ALU.add, [n.res, ds_.res], [ds_.res])
        pg.s.op("dve", lambda e: e.reciprocal(out=ds_.ap[:, 4:8], in_=ds_.ap[:, 0:4]), [ds_.res], [ds_.res])
        for g in range(3):
            TT(pg, "dve", a.ap[:, g * 256:(g + 1) * 256].rearrange("p (h e) -> p h e", e=64), n4[:, g, :, 0:64],
               ds_.ap[:, 4:8].unsqueeze(2).broadcast_to([P, 4, 64]), ALU.mult, [n.res, ds_.res], [a.res])
    phase_outproj(pg, NT, fill, "x4", x4, 0, wout, rout, "x5", x5)


def build_L4():
    pg = Prog()
    c = common_inputs(pg)
    KTX = pg.dt("KTX", [768, HALO + TOK], BF16, "ExternalInput")
    VDX = pg.dt("VDX", [HALO + TOK, 768], BF16, "ExternalInput")
    QTD = pg.dt("QTD", [768, TOK], BF16, "ExternalInput")
    HM1 = pg.dt("HM1", [TOK, 256], BF16, "ExternalInput")
    x4 = pg.dt("x4", [TOK, D], F32, "ExternalInput")
    relb = pg.dt("relb", [32, 12], F32, "ExternalInput")
    oh_d = pg.dt("oh", [3, 64, 384], F32, "ExternalInput")
    flag_d = pg.dt("flag", [P, 1], F32, "ExternalInput")
    jrev_d = pg.dt("jrev", [P, P], F32, "ExternalInput")
    w_out = pg.dt("w_out", [D, D], F32, "ExternalInput")
    wg = pg.dt("wg", [D, FF], F32, "ExternalInput")
    wu = pg.dt("wu", [D, FF], F32, "ExternalInput")
    wd = pg.dt("wd", [FF, D], F32, "ExternalInput")
    g2 = pg.dt("g_ffn", [1, D], F32, "ExternalInput")
    BV = pg.dt("BV", [3, 12, 384], F32, "Internal")
    NUMX = pg.dt("NUMX", [TOK, 780], F32, "Internal")
    x5 = pg.dt("x5", [TOK, D], F32, "Internal")
    out = pg.dt("out", [TOK, D], F32, "ExternalOutput")
    setup_consts(pg, c["idn"])
    pg.phase()
    phase_dilated(pg, KTX, VDX, QTD, relb, oh_d, flag_d, jrev_d, BV, NUMX)
    pg.phase()
    wo = load_w(pg, 1, w_out, 0, D, 0)
    ffw = load_ffn_weights(pg, 0, wg, wu, wd, *FFN_PARTS[0])
    phase_dil_out(pg, NUMX, HM1, x4, wo[0], wo[1], x5)
    ffn_full(pg, wg, wu, wd, g2, "x5", x5, "out", out, NT, first_loaded=ffw)
    return pg.finish()


def run_L4(inp, l3, cores=range(NCORE)):
    import ml_dtypes
    nc = build_L4()
    cs = _consts()
    oh = _bias_onehot()
    maps = []
    for c in cores:
        r = l3[c]
        KTX = np.zeros((768, HALO + TOK), ml_dtypes.bfloat16)
        VDX = np.zeros((HALO + TOK, 768), ml_dtypes.bfloat16)
        KTX[:, HALO:] = np.asarray(r["KTD"])
        VDX[HALO:] = np.asarray(r["VD"])
        has_prev = (c % 4) > 0
        if has_prev:
            KTX[:, :HALO] = np.asarray(l3[c - 1]["KTD"])[:, TOK - HALO:]
            VDX[:HALO] = np.asarray(l3[c - 1]["VD"])[TOK - HALO:]
        flag = np.full((P, 1), 1.0 if has_prev else 0.0, np.float32)
        maps.append(dict(cs, KTX=KTX, VDX=VDX, QTD=r["QTD"], HM1=r["HM1"], x4=r["x4"], relb=inp["rel_bias"],
                         oh=oh, flag=flag, jrev=np.ascontiguousarray(np.eye(P, dtype=np.float32)[::-1]), w_out=inp["b_w_out"][0], wg=inp["ffn2_w_gate"][1],
                         wu=inp["ffn2_w_up"][1], wd=inp["ffn2_w_down"][1], g_ffn=inp["ffn2_norm"][1:2]))
    maps = [{k: np.ascontiguousarray(v) for k, v in m.items()} for m in maps]
    return run_bass_kernel_spmd(nc, maps, core_ids=list(range(len(maps)))).results

SEQ = 16384
NCH = SEQ // P
DV = 192


def build_L2(nchunks=NCH):
    pg = Prog()
    c = common_inputs(pg)
    qT_d = pg.dt("qT", [2, 96, SEQ], BF16, "ExternalInput")
    kT_d = pg.dt("kT", [2, 96, SEQ], BF16, "ExternalInput")
    kt_d = pg.dt("ktok", [SEQ, DV], BF16, "ExternalInput")
    v_d = pg.dt("v", [SEQ, DV], BF16, "ExternalInput")
    gv_d = pg.dt("gv", [SEQ, 3], F32, "ExternalInput")
    cell_d = pg.dt("cell", [SEQ, DV], F32, "ExternalOutput")
    setup_consts(pg, c["idn"])
    s = pg.s
    pg.phase()
    tri = pg.tf(P, "tri")
    LD(pg, tri, tri.ap, c["tri"], [])
    C = pg.tf(2 * 193, "C", ("p (a e) -> p a e", dict(a=2)))
    gv = Ring([pg.tf(12, "gv%d" % i, ("p (a e) -> p a e", dict(a=4))) for i in range(2)])
    cell = Ring([pg.tf(4 * DV, "cell%d" % i, ("p (a e) -> p a e", dict(a=4))) for i in range(2)])
    dn = Ring([pg.tf(2, "dn%d" % i) for i in range(2)])
    qT = Ring([pg.tb(1024, "qT%d" % i, ("p (a t) -> p a t", dict(a=2))) for i in range(2)])
    kT = Ring([pg.tb(1024, "kT%d" % i, ("p (a t) -> p a t", dict(a=2))) for i in range(2)])
    kt = Ring([pg.tb(4 * DV, "kt%d" % i, ("p (a e) -> p a e", dict(a=4))) for i in range(2)])
    vx = Ring([pg.tb(4 * 193, "vx%d" % i, ("p (a e) -> p a e", dict(a=4))) for i in range(2)])
    wt = Ring([pg.tb(P, "wt%d" % i) for i in range(2)])
    kw = Ring([pg.tb(DV, "kw%d" % i) for i in range(2)])
    Cb = pg.tb(2 * 193, "Cb", ("p (a e) -> p a e", dict(a=2)))
    ps_s = Ring([pg.ps[0], pg.ps[1]])
    ps_n = Ring([pg.ps[2], pg.ps[3]])
    ps_u = [pg.ps[4], pg.ps[5]]
    s.op("dve", lambda e: e.memset(C.ap, 0.0), [], [C.res])
    for t in vx.items:
        s.op("pool", lambda e, t=t: e.memset(t.ap, 1.0), [], [t.res])
    s.op("act", lambda e: e.copy(out=Cb.ap[0:96], in_=C.ap[0:96]), [C.res], [Cb.res])
    for g0 in range(0, nchunks, 4):
        t0 = g0 * P
        q, k, ktk, v, g, ce = qT.next(), kT.next(), kt.next(), vx.next(), gv.next(), cell.next()
        LD(pg, q, q.ap[0:96], qT_d[:, :, t0:t0 + 512].rearrange("a p t -> p a t"), [])
        LD(pg, k, k.ap[0:96], kT_d[:, :, t0:t0 + 512].rearrange("a p t -> p a t"), [])
        LD(pg, ktk, ktk.ap, kt_d[t0:t0 + 512, :].rearrange("(a p) e -> p a e", p=P), [])
        LD(pg, v, v.ap[:, :, 0:DV], v_d[t0:t0 + 512, :].rearrange("(a p) e -> p a e", p=P), [])
        LD(pg, g, g.ap, gv_d[t0:t0 + 512, :].rearrange("(a p) e -> p a e", p=P), [])
        for a in range(4):
            cs = slice(a * P, (a + 1) * P)
            pss = ps_s.next()
            for hf in range(2):
                MM(pg, pss, pss.ap[:, 0:P], k.ap[0:96, hf, cs], q.ap[0:96, hf, cs], [k.res, q.res], hf == 0, hf == 1)
            w = wt.next()
            STT(pg, w.ap, pss.ap[:, 0:P], g.ap[:, a, 0:1], tri.ap, ALU.mult, ALU.mult, [pss.res, g.res, tri.res],
                [w.res])
            psn = ps_n.next()
            MM(pg, psn, psn.ap[:, 0:193], w.ap, v.ap[:, a, :], [w.res, v.res], True, False)
            for hf in range(2):
                MM(pg, psn, psn.ap[:, 0:193], q.ap[0:96, hf, cs], Cb.ap[0:96, hf, :], [q.res, Cb.res], False, hf == 1)
            d = dn.next()
            TS(pg, "dve", d.ap[:, 0:1], psn.ap[:, 192:193], 1.0, None, ALU.mult, None, [psn.res], [d.res])
            STT(pg, d.ap[:, 0:1], d.ap[:, 0:1], -1.0, d.ap[:, 0:1], ALU.mult, ALU.max, [d.res], [d.res])
            TT(pg, "dve", d.ap[:, 0:1], d.ap[:, 0:1], g.ap[:, a, 1:2], ALU.max, [d.res, g.res], [d.res])
            s.op("dve", lambda e, d=d: e.reciprocal(out=d.ap[:, 1:2], in_=d.ap[:, 0:1]), [d.res], [d.res])
            TS(pg, "dve", ce.ap[:, a, :], psn.ap[:, 0:DV], d.ap[:, 1:2], None, ALU.mult, None, [psn.res, d.res], [ce.res])
            kk = kw.next()
            TS(pg, "pool", kk.ap, ktk.ap[:, a, :], g.ap[:, a, 0:1], None, ALU.mult, None, [ktk.res, g.res], [kk.res])
            for hf in range(2):
                MM(pg, ps_u[hf], ps_u[hf].ap[0:96, 0:193], kk.ap[:, hf * 96:(hf + 1) * 96], v.ap[:, a, :],
                   [kk.res, v.res], True, True)
            TS(pg, "dve", C.ap[0:96], C.ap[0:96], g.ap[0:96, a, 2:3], None, ALU.mult, None, [C.res, g.res], [C.res])
            for hf in range(2):
                STT(pg, C.ap[0:96, hf, :], ps_u[hf].ap[0:96, 0:193], g.ap[0:96, a, 2:3], C.ap[0:96, hf, :],
                    ALU.mult, ALU.add, [ps_u[hf].res, g.res, C.res], [C.res])
            s.op("act", lambda e: e.copy(out=Cb.ap[0:96], in_=C.ap[0:96]), [C.res], [Cb.res])
        ST(pg, cell_d[t0:t0 + 512, :].rearrange("(a p) e -> p a e", p=P), ce, ce.ap, [pg.dr("cell", g0)])
    return pg.finish()


def run_L2(l1, cores=range(NCORE), nchunks=NCH):
    nc = build_L2(nchunks)
    cs = _consts()
    maps = []
    for c in cores:
        b, h = c // 4, c % 4
        rs = [l1[b * 4 + j] for j in range(4)]
        QKT = np.concatenate([np.asarray(r["QKT"]) for r in rs], axis=2)
        KT = np.concatenate([np.asarray(r["KTOK"]) for r in rs], axis=0)
        VT = np.concatenate([np.asarray(r["VTOK"]) for r in rs], axis=0)
        GV = np.concatenate([np.asarray(r["GV"]) for r in rs], axis=0)
        maps.append(dict(cs, qT=QKT[2 * h:2 * h + 2], kT=QKT[8 + 2 * h:8 + 2 * h + 2],
                         ktok=KT[:, h * DV:(h + 1) * DV], v=VT[:, h * DV:(h + 1) * DV],
                         gv=GV[:, [h, 4 + h, 8 + h]]))
    maps = [{k: np.ascontiguousarray(v) for k, v in m.items()} for m in maps]
    return run_bass_kernel_spmd(nc, maps, core_ids=list(range(len(maps)))).results


def kernel(**inputs):
    inp = {k: np.asarray(v) for k, v in inputs.items()}
    l1 = run_L1(inp, list(range(NCORE)))
    l2 = run_L2(l1)
    l3 = run_L3(inp, l1, l2)
    del l2
    l4 = run_L4(inp, l3)
    out = np.empty((2, SEQ, D), np.float32)
    for c in range(NCORE):
        b, t0 = core_tokens(c)
        out[b, t0:t0 + TOK] = np.asarray(l4[c]["out"])
    return out
```
